# Optimizing a Trainium2 kernel written in Bass

```python
import jax, jax.numpy as jnp
from jax import lax
import numpy as np

D_MODEL = 1024
BATCH = 8
SEQ = 4096
DEPTH = 2
DEC_BATCH = 2
DEC_SEQ = 8192
PAST_LEN = 128

N_META = 16
GRID_W = 64
D_FF = 2816
EPS = 1e-6
Q_BLOCK = 128
NEG_INF = -1e30
MLA_HEADS = 8
MLA_Q_LORA = 256
MLA_KV_LORA = 128
MLA_NOPE = 64
MLA_ROPE = 32
MLA_V = 64
MLA_THETA = 10000.0
GQA_HEADS = 8
GQA_KV_HEADS = 2
GQA_HEAD_DIM = 64
AXIAL_THETA = 10000.0
NA_HEADS = 16
NA_HEAD_DIM = D_MODEL // NA_HEADS
NA_WIN_R = 8
NA_WIN_C = 16
N_EVEN = (DEPTH + 1) // 2
N_ODD = DEPTH // 2
IN_SPLITS = (MLA_Q_LORA, MLA_KV_LORA, MLA_ROPE, GQA_HEADS * GQA_HEAD_DIM,
             GQA_KV_HEADS * GQA_HEAD_DIM, GQA_KV_HEADS * GQA_HEAD_DIM)
IN_COLS = sum(IN_SPLITS)
MIX_WIDTH = MLA_HEADS * MLA_V + GQA_HEADS * GQA_HEAD_DIM

kernel_name = "hybrid_mla_gqa_natten_macaron_encoder"


def rmsnorm(x, g):
    x32 = x.astype(jnp.float32)
    y = x32 * lax.rsqrt(jnp.mean(x32 * x32, axis=-1, keepdims=True) + EPS)
    return (y * g.astype(jnp.float32)).astype(x.dtype)


def swiglu(x, wg, wu, wd):
    return (jax.nn.silu(x @ wg) * (x @ wu)) @ wd


def rope(x, pos, theta):
    half = x.shape[-1] // 2
    inv = 1.0 / (theta ** (jnp.arange(half, dtype=jnp.float32) / half))
    ang = pos.astype(jnp.float32)[:, None] * inv[None, :]
    cos = jnp.cos(ang)[:, None, :]
    sin = jnp.sin(ang)[:, None, :]
    x32 = x.astype(jnp.float32)
    x1, x2 = x32[..., :half], x32[..., half:]
    return jnp.concatenate([x1 * cos - x2 * sin, x2 * cos + x1 * sin], axis=-1).astype(x.dtype)


def blocked_attention(q, k, v, scale):
    b, L, hk, g, dk = q.shape
    nb = -(-L // Q_BLOCK)
    lp = nb * Q_BLOCK
    qp = jnp.pad(q, ((0, 0), (0, lp - L), (0, 0), (0, 0), (0, 0)))
    qb = jnp.moveaxis(qp.reshape(b, nb, Q_BLOCK, hk, g, dk), 1, 0)

    def one_block(qi):
        s = jnp.einsum('bqhgd,bkhd->bhgqk', qi, k).astype(jnp.float32) * scale
        p = jax.nn.softmax(s, axis=-1).astype(v.dtype)
        return jnp.einsum('bhgqk,bkhe->bqhge', p, v)

    o = lax.map(one_block, qb)
    return jnp.moveaxis(o, 0, 1).reshape(b, lp, hk, g, v.shape[-1])[:, :L]


def mla_gqa_mixer(h, w_in, q_norm, w_uq, kv_norm, w_ukv, gq_norm, gk_norm, w_out):
    b, L, _ = h.shape
    n_tok = L - N_META
    proj = h @ w_in
    cuts = np.cumsum(IN_SPLITS)[:-1].tolist()
    cq, ckv, kr, qb_, kb_, vb_ = jnp.split(proj, cuts, axis=-1)
    pos = jnp.arange(L, dtype=jnp.float32)
    qa = (rmsnorm(cq, q_norm) @ w_uq).reshape(b, L, MLA_HEADS, MLA_NOPE + MLA_ROPE)
    qa = jnp.concatenate([qa[..., :MLA_NOPE], rope(qa[..., MLA_NOPE:], pos, MLA_THETA)], axis=-1)
    kv = (rmsnorm(ckv, kv_norm) @ w_ukv).reshape(b, L, MLA_HEADS, MLA_NOPE + MLA_V)
    k_rope = jnp.broadcast_to(rope(kr[:, :, None, :], pos, MLA_THETA), (b, L, MLA_HEADS, MLA_ROPE))
    ka = jnp.concatenate([kv[..., :MLA_NOPE], k_rope], axis=-1)
    o_a = blocked_attention(qa[:, :, :, None, :], ka, kv[..., MLA_NOPE:],
                            (MLA_NOPE + MLA_ROPE) ** -0.5)
    tok = jnp.arange(n_tok)
    row = jnp.concatenate([jnp.full((N_META,), -1.0, jnp.float32), (tok // GRID_W).astype(jnp.float32)])
    col = jnp.concatenate([jnp.arange(N_META, dtype=jnp.float32), (tok % GRID_W).astype(jnp.float32)])
    half = GQA_HEAD_DIM // 2

    def axial(x):
        return jnp.concatenate([rope(x[..., :half], row, AXIAL_THETA),
                                rope(x[..., half:], col, AXIAL_THETA)], axis=-1)

    qg = axial(rmsnorm(qb_.reshape(b, L, GQA_HEADS, GQA_HEAD_DIM), gq_norm))
    qg = qg.reshape(b, L, GQA_KV_HEADS, GQA_HEADS // GQA_KV_HEADS, GQA_HEAD_DIM)
    kg = axial(rmsnorm(kb_.reshape(b, L, GQA_KV_HEADS, GQA_HEAD_DIM), gk_norm))
    vg = vb_.reshape(b, L, GQA_KV_HEADS, GQA_HEAD_DIM)
    o_b = blocked_attention(qg, kg, vg, GQA_HEAD_DIM ** -0.5)
    o = jnp.concatenate([o_a.reshape(b, L, -1), o_b.reshape(b, L, -1)], axis=-1)
    return o @ w_out


def neighbourhood_mixer(h, w_qkv, rpb, meta_bias, w_out):
    b, L, _ = h.shape
    n_tok = L - N_META
    rows = n_tok // GRID_W
    win_r = min(NA_WIN_R, rows)
    kblk_len = win_r * GRID_W
    qkv = (h @ w_qkv).reshape(b, L, 3, NA_HEADS, NA_HEAD_DIM)
    q = qkv[:, :, 0] * (NA_HEAD_DIM ** -0.5)
    k = qkv[:, :, 1]
    v = qkv[:, :, 2]
    qm, km, vm = q[:, :N_META], k[:, :N_META], v[:, :N_META]
    grid = (b, rows, GRID_W, NA_HEADS, NA_HEAD_DIM)
    qg = q[:, N_META:].reshape(grid)
    kg = k[:, N_META:].reshape(grid)
    vg = v[:, N_META:].reshape(grid)
    r_idx = jnp.arange(rows)
    r_start = jnp.clip(r_idx - win_r // 2, 0, rows - win_r)
    c_idx = jnp.arange(GRID_W)
    c_start = jnp.clip(c_idx - NA_WIN_C // 2, 0, GRID_W - NA_WIN_C)
    col_mask = (c_idx[None, :] >= c_start[:, None]) & (c_idx[None, :] < c_start[:, None] + NA_WIN_C)
    blk_mask = jnp.tile(col_mask, (1, win_r))
    col_off = jnp.clip(c_idx[None, :] - c_idx[:, None] + NA_WIN_C - 1, 0, 2 * NA_WIN_C - 2)
    rpb_c = rpb[:, :, col_off]
    mb = meta_bias[:, None, :].astype(jnp.float32)

    def one_row(args):
        q_r, r = args
        rs = r_start[r]
        k_blk = lax.dynamic_slice_in_dim(kg, rs, win_r, axis=1).reshape(b, kblk_len, NA_HEADS, NA_HEAD_DIM)
        v_blk = lax.dynamic_slice_in_dim(vg, rs, win_r, axis=1).reshape(b, kblk_len, NA_HEADS, NA_HEAD_DIM)
        row_off = rs + jnp.arange(win_r) - r + NA_WIN_R - 1
        bias = jnp.take(rpb_c, row_off, axis=1).transpose(0, 2, 1, 3).reshape(NA_HEADS, GRID_W, kblk_len)
        s_g = jnp.einsum('bqhd,bkhd->bhqk', q_r, k_blk).astype(jnp.float32) + bias.astype(jnp.float32)
        s_g = jnp.where(blk_mask, s_g, NEG_INF)
        s_m = jnp.einsum('bqhd,bmhd->bhqm', q_r, km).astype(jnp.float32) + mb
        p = jax.nn.softmax(jnp.concatenate([s_g, s_m], axis=-1), axis=-1).astype(v.dtype)
        return (jnp.einsum('bhqk,bkhd->bqhd', p[..., :kblk_len], v_blk)
                + jnp.einsum('bhqm,bmhd->bqhd', p[..., kblk_len:], vm))

    o_g = lax.map(one_row, (jnp.moveaxis(qg, 1, 0), r_idx))
    o_g = jnp.moveaxis(o_g, 0, 1).reshape(b, n_tok, NA_HEADS * NA_HEAD_DIM)
    s_mm = jnp.einsum('bqhd,bmhd->bhqm', qm, km).astype(jnp.float32) + mb
    p_mm = jax.nn.softmax(s_mm, axis=-1).astype(v.dtype)
    o_m = jnp.einsum('bhqm,bmhd->bqhd', p_mm, vm).reshape(b, N_META, NA_HEADS * NA_HEAD_DIM)
    return jnp.concatenate([o_m, o_g], axis=1) @ w_out


def encoder(x, meta, norm_gains, ffn1_w_gate, ffn1_w_up, ffn1_w_down, ffn2_w_gate, ffn2_w_up,
            ffn2_w_down, attn_w_in, mla_q_norm, mla_w_uq, mla_kv_norm, mla_w_ukv, gqa_q_norm,
            gqa_k_norm, attn_w_out, na_w_qkv, na_rpb, na_meta_bias, na_w_out):
    b = x.shape[0]
    h = jnp.concatenate([jnp.broadcast_to(meta.astype(x.dtype)[None], (b, N_META, D_MODEL)), x], axis=1)
    for i in range(DEPTH):
        g = norm_gains[i]
        h = h + 0.5 * rmsnorm(swiglu(rmsnorm(h, g[0]), ffn1_w_gate[i], ffn1_w_up[i], ffn1_w_down[i]), g[1])
        a = rmsnorm(h, g[2])
        j = i // 2
        if i % 2 == 0:
            m = mla_gqa_mixer(a, attn_w_in[j], mla_q_norm[j], mla_w_uq[j], mla_kv_norm[j], mla_w_ukv[j],
                              gqa_q_norm[j], gqa_k_norm[j], attn_w_out[j])
        else:
            m = neighbourhood_mixer(a, na_w_qkv[j], na_rpb[j], na_meta_bias[j], na_w_out[j])
        h = h + rmsnorm(m, g[3])
        h = h + 0.5 * rmsnorm(swiglu(rmsnorm(h, g[4]), ffn2_w_gate[i], ffn2_w_up[i], ffn2_w_down[i]), g[5])
    return h[:, N_META:]


def setup_inputs(seed: int = 0) -> dict:
    key = jax.random.key(seed)
    ks = jax.random.split(key, 24)

    def nrm(k, shape, scale):
        return jax.random.normal(k, shape, jnp.float32) * scale

    def gain(k, shape):
        return 1.0 + 0.05 * jax.random.normal(k, shape, jnp.float32)

    D, F = D_MODEL, D_FF
    return {
        "x_prompt": nrm(ks[0], (BATCH, SEQ, D), 1.0),
        "x_sample": nrm(ks[1], (DEC_BATCH, DEC_SEQ, D), 1.0),
        "meta": nrm(ks[2], (N_META, D), 1.0),
        "norm_gains": gain(ks[3], (DEPTH, 6, D)),
        "ffn1_w_gate": nrm(ks[4], (DEPTH, D, F), D ** -0.5),
        "ffn1_w_up": nrm(ks[5], (DEPTH, D, F), D ** -0.5),
        "ffn1_w_down": nrm(ks[6], (DEPTH, F, D), F ** -0.5),
        "ffn2_w_gate": nrm(ks[7], (DEPTH, D, F), D ** -0.5),
        "ffn2_w_up": nrm(ks[8], (DEPTH, D, F), D ** -0.5),
        "ffn2_w_down": nrm(ks[9], (DEPTH, F, D), F ** -0.5),
        "attn_w_in": nrm(ks[10], (N_EVEN, D, IN_COLS), D ** -0.5),
        "mla_q_norm": gain(ks[11], (N_EVEN, MLA_Q_LORA)),
        "mla_w_uq": nrm(ks[12], (N_EVEN, MLA_Q_LORA, MLA_HEADS * (MLA_NOPE + MLA_ROPE)), MLA_Q_LORA ** -0.5),
        "mla_kv_norm": gain(ks[13], (N_EVEN, MLA_KV_LORA)),
        "mla_w_ukv": nrm(ks[14], (N_EVEN, MLA_KV_LORA, MLA_HEADS * (MLA_NOPE + MLA_V)), MLA_KV_LORA ** -0.5),
        "gqa_q_norm": gain(ks[15], (N_EVEN, GQA_HEAD_DIM)),
        "gqa_k_norm": gain(ks[16], (N_EVEN, GQA_HEAD_DIM)),
        "attn_w_out": nrm(ks[17], (N_EVEN, MIX_WIDTH, D), MIX_WIDTH ** -0.5),
        "na_w_qkv": nrm(ks[18], (N_ODD, D, 3 * NA_HEADS * NA_HEAD_DIM), D ** -0.5),
        "na_rpb": nrm(ks[19], (N_ODD, NA_HEADS, 2 * NA_WIN_R - 1, 2 * NA_WIN_C - 1), 0.1),
        "na_meta_bias": nrm(ks[20], (N_ODD, NA_HEADS, N_META), 0.1),
        "na_w_out": nrm(ks[21], (N_ODD, NA_HEADS * NA_HEAD_DIM, D), (NA_HEADS * NA_HEAD_DIM) ** -0.5),
    }


def reference(x_prompt, x_sample, meta, norm_gains, ffn1_w_gate, ffn1_w_up, ffn1_w_down, ffn2_w_gate,
              ffn2_w_up, ffn2_w_down, attn_w_in, mla_q_norm, mla_w_uq, mla_kv_norm, mla_w_ukv,
              gqa_q_norm, gqa_k_norm, attn_w_out, na_w_qkv, na_rpb, na_meta_bias, na_w_out):
    y_prompt = encoder(x_prompt, meta, norm_gains, ffn1_w_gate, ffn1_w_up, ffn1_w_down, ffn2_w_gate,
                       ffn2_w_up, ffn2_w_down, attn_w_in, mla_q_norm, mla_w_uq, mla_kv_norm, mla_w_ukv,
                       gqa_q_norm, gqa_k_norm, attn_w_out, na_w_qkv, na_rpb, na_meta_bias, na_w_out)
    y_sample = encoder(x_sample, meta, norm_gains, ffn1_w_gate, ffn1_w_up, ffn1_w_down, ffn2_w_gate,
                       ffn2_w_up, ffn2_w_down, attn_w_in, mla_q_norm, mla_w_uq, mla_kv_norm, mla_w_ukv,
                       gqa_q_norm, gqa_k_norm, attn_w_out, na_w_qkv, na_rpb, na_meta_bias, na_w_out)
    return (y_prompt, y_sample)
```

```python
from contextlib import ExitStack

import numpy as np
import ml_dtypes

import concourse.bass as bass
import concourse.mybir as mybir
from concourse.bass_utils import run_bass_kernel_spmd

F32 = mybir.dt.float32
BF16 = mybir.dt.bfloat16
AF = mybir.ActivationFunctionType
ALU = mybir.AluOpType

D = 1024
DC = 8
FF = 2816
FC = 22
NMETA = 16
GW = 64
EPS = 1e-6
IN_COLS = 1184
TG = 512
SEM_EPOCH = 30000
NCORES = 8


class Op:
    __slots__ = ("sem", "cnt", "vc", "eng", "is_dma")

    def __init__(self, sem, cnt, vc, eng, is_dma):
        self.sem = sem
        self.cnt = cnt
        self.vc = vc
        self.eng = eng
        self.is_dma = is_dma


class Sched:
    def __init__(self, nc, stack):
        self.nc = nc
        self.stack = stack
        self.eng = {"pe": nc.tensor, "act": nc.scalar, "dve": nc.vector, "pool": nc.gpsimd, "sp": nc.sync}
        self.sem_h = {}
        self.sem_cnt = {}
        self.eng_n = {e: 0 for e in self.eng}
        self.known = {e: {} for e in self.eng}
        self.last_writer = {}
        self.readers = {}
        self.n_wait = 0
        self.n_ops = 0

    def _sem(self, name):
        h = self.sem_h.get(name)
        if h is None:
            h = self.stack.enter_context(self.nc.semaphore(name))
            self.sem_h[name] = h
            self.sem_cnt[name] = 0
        return h

    def _need(self, eng, d):
        kn = self.known[eng]
        if kn.get(d.sem, 0) >= d.cnt:
            return
        self.eng[eng].wait_ge(self._sem(d.sem), d.cnt)
        self.n_wait += 1
        kn = dict(kn)
        kn[d.sem] = d.cnt
        for s, v in d.vc.items():
            if kn.get(s, 0) < v:
                kn[s] = v
        self.known[eng] = kn

    def op(self, eng, fn, reads=(), writes=(), dma_sem=None):
        is_dma = dma_sem is not None
        deps = {}

        def add(d, raw):
            if d is None:
                return
            if not d.is_dma and d.eng == eng and not is_dma:
                if eng == "pe":
                    return
            cur = deps.get(d.sem)
            if cur is None or cur.cnt < d.cnt:
                deps[d.sem] = d

        for k in reads:
            add(self.last_writer.get(k), True)
        for k in writes:
            add(self.last_writer.get(k), False)
            rd = self.readers.get(k)
            if rd:
                for r in rd.values():
                    add(r, False)
        for d in deps.values():
            self._need(eng, d)
        if is_dma:
            sem = dma_sem
            inc = 16
            prev = self.sem_cnt.get(sem, 0)
            assert self.known[eng].get(sem, 0) >= prev, f"dma sem {sem} reused while in flight"
        else:
            sem = f"{eng}{self.eng_n[eng] // SEM_EPOCH}"
            self.eng_n[eng] += 1
            inc = 1
        h = self._sem(sem)
        ins = fn(self.eng[eng])
        ins.then_inc(h, inc)
        self.sem_cnt[sem] += inc
        o = Op(sem, self.sem_cnt[sem], self.known[eng], eng, is_dma)
        for k in reads:
            self.readers.setdefault(k, {})[sem] = o
        for k in writes:
            self.last_writer[k] = o
            self.readers[k] = {}
        self.n_ops += 1
        return o

    def barrier(self):
        final = {s: c for s, c in self.sem_cnt.items() if c > 0 and not s.startswith("wc")}
        for e in self.eng:
            kn = dict(self.known[e])
            for s, c in final.items():
                if kn.get(s, 0) < c:
                    self.eng[e].wait_ge(self._sem(s), c)
                    kn[s] = c
            self.known[e] = kn
        self.last_writer = {k: v for k, v in self.last_writer.items() if isinstance(k, tuple) and k[0] == "wdram"}
        self.readers = {}

    def finish(self):
        for s, c in self.sem_cnt.items():
            if c > 0 and self.known["sp"].get(s, 0) < c:
                self.eng["sp"].wait_ge(self._sem(s), c)


def split_groups(n, maxg=TG):
    ng = -(-n // maxg)
    sz = -(-n // ng)
    sz = -(-sz // 16) * 16
    out = []
    s = 0
    while s < n:
        out.append((s, min(sz, n - s)))
        s += sz
    return out


def tiles128(n):
    return [(s, min(128, n - s)) for s in range(0, n, 128)]


class Builder:
    def __init__(self, rows_p, rows_s):
        self.rows_p = rows_p
        self.rows_s = rows_s
        self.rq = rows_s // 4
        self.wr = self.rq + 8
        self.LP = NMETA + rows_p * GW
        self.LS = NMETA + rows_s * GW
        self.NQS = NMETA + self.wr * GW
        self.jobs = {
            "P": dict(L=self.LP, nq=self.LP, rows=rows_p),
            "S": dict(L=self.LS, nq=self.NQS, rows=self.wr),
        }
        self.nc = bass.Bass("TRN2", target_bir_lowering=False)
        self.dsem_n = 0
        self.semmap = {}

    def din(self, name, shape, dt=F32):
        return self.nc.dram_tensor(name, list(shape), dt, kind="ExternalInput").ap()

    def dout(self, name, shape, dt=F32):
        return self.nc.dram_tensor(name, list(shape), dt, kind="ExternalOutput").ap()

    def dscr(self, name, shape, dt):
        return self.nc.dram_tensor(name, list(shape), dt, kind="Internal").ap()

    def build(self):
        nc = self.nc
        with ExitStack() as st:
            self.S = Sched(nc, st)
            self.gst = st
            self.declare_io()
            self.pp = [st.enter_context(nc.psum_tensor(f"pp{i}", [128, 2, 512], F32)) for i in range(4)]
            self.psb = [self.pp[i // 2][:, i % 2, :] for i in range(8)]
            self.consts()
            self.cast_weights()
            for job in ("P", "S"):
                self.run_job(job)
            self.S.barrier()
            self.S.finish()
        return nc

    def declare_io(self):
        J = self.jobs
        self.x = {"P": self.din("xP", [self.LP, D]), "S": self.din("xS", [self.LS, D])}
        self.tab = {}
        for j in J:
            for t in ("cosM", "sinM", "cosA", "sinA"):
                self.tab[(j, t)] = self.din(f"{t}{j}", [128, J[j]["L"]])
        self.y = {"P": self.dout("yP", [self.rows_p * GW, D]), "S": self.dout("yS", [self.wr * GW, D])}
        self.wspec = {}
        for l in range(2):
            for w in (1, 2):
                self.wspec[f"g{l}{w}"] = (D, FF)
                self.wspec[f"u{l}{w}"] = (D, FF)
                self.wspec[f"d{l}{w}"] = (FF, D)
        self.wspec.update(win=(D, IN_COLS), wuq=(256, 768), wukv=(128, 1024), wout=(D, D), wqkv=(D, 3072), wnout=(D, D))
        self.w32 = {k: self.din("w32_" + k, s) for k, s in self.wspec.items()}
        self.wbf = {k: self.dscr("wbf_" + k, s, BF16) for k, s in self.wspec.items()}
        self.c_ident = self.din("c_ident", [128, 128])
        self.c_rm = self.din("c_rm", [128, 128])
        self.c_ra = self.din("c_ra", [128, 128])
        self.c_gains = self.din("c_gains", [128, 12 * 8])
        self.c_small = self.din("c_small", [128, 8])
        self.c_bt = self.din("c_bt", [16, 128, 16 * 64])
        self.c_mb = self.din("c_mb", [16, 16])
        self.scr = {}
        for j in J:
            L, nq = J[j]["L"], J[j]["nq"]
            self.scr[j] = dict(
                H=self.dscr(f"H{j}", [D, L], F32),
                QA=self.dscr(f"QA{j}", [768, nq], BF16),
                CKV=self.dscr(f"CKV{j}", [128, L], BF16),
                KR=self.dscr(f"KR{j}", [32, L], BF16),
                QG=self.dscr(f"QG{j}", [512, nq], BF16),
                KG=self.dscr(f"KG{j}", [128, L], BF16),
                VG=self.dscr(f"VG{j}", [L, 128], BF16),
                O=self.dscr(f"O{j}", [D, nq], BF16),
                Q1=self.dscr(f"Q1{j}", [D, nq], BF16),
                K1=self.dscr(f"K1{j}", [D, nq], BF16),
                V1=self.dscr(f"V1{j}", [nq, D], BF16),
            )

    def sb(self, st, name, shape, dt):
        self.sb_n = getattr(self, "sb_n", 0) + 1
        return st.enter_context(self.nc.sbuf_tensor(f"{name}_{self.sb_n}", list(shape), dt))

    def dma(self, out, in_, reads=(), writes=(), eng="sp", sem=None):
        if sem is None:
            sem = f"dq{eng}{self.dsem_n % 8}"
            self.dsem_n += 1
            prev = self.S.sem_cnt.get(sem, 0)
            if self.S.known[eng].get(sem, 0) < prev:
                self.S.eng[eng].wait_ge(self.S._sem(sem), prev)
                kn = dict(self.S.known[eng])
                kn[sem] = prev
                self.S.known[eng] = kn
        else:
            m = self.semmap
            if sem not in m:
                m[sem] = f"ds{len(m)}"
            sem = m[sem]
        return self.S.op(eng, lambda e: e.dma_start(out=out, in_=in_), reads=reads, writes=writes, dma_sem=sem)

    def stage_begin(self):
        self.S.barrier()
        self.semmap = {}

    def consts(self):
        st = self.gst
        S = self.S
        self.ident = self.sb(st, "ident", [128, 128], F32)
        self.identb = self.sb(st, "identb", [128, 128], BF16)
        self.rm = self.sb(st, "rm", [128, 128], F32)
        self.gains = self.sb(st, "gains", [128, 96], F32)
        self.hgains = self.sb(st, "hgains", [128, 96], F32)
        self.small = self.sb(st, "small", [128, 8], F32)
        self.onesb = self.sb(st, "onesb", [128, 128], BF16)
        self.blkones = self.sb(st, "blkones", [128, 128], BF16)
        self.epsc = self.sb(st, "epsc", [128, 1], F32)
        self.dma(self.ident[:], self.c_ident, writes=["ident"])
        self.dma(self.rm[:], self.c_rm, writes=["rm"])
        self.dma(self.gains[:], self.c_gains, writes=["gains"])
        self.dma(self.small[:], self.c_small, writes=["small"])
        S.op("dve", lambda e: e.tensor_copy(out=self.identb[:], in_=self.ident[:]), reads=["ident"], writes=["identb"])
        S.op("dve", lambda e: e.tensor_scalar(out=self.hgains[:], in0=self.gains[:], scalar1=0.5, scalar2=None, op0=ALU.mult),
             reads=["gains"], writes=["hgains"])
        S.op("pool", lambda e: e.memset(self.onesb[:], 1.0), writes=["onesb"])
        S.op("pool", lambda e: e.memset(self.blkones[:], 0.0), writes=["blkones"])
        S.op("pool", lambda e: e.memset(self.blkones[0:64, 0:64], 1.0), writes=["blkones"])
        S.op("pool", lambda e: e.memset(self.blkones[64:128, 64:128], 1.0), writes=["blkones"])
        S.op("pool", lambda e: e.memset(self.epsc[:], EPS), writes=["epsc"])
        self.ckeys = ["ident", "identb", "rm", "ra", "gains", "hgains", "small", "onesb", "blkones", "ones32", "epsc"]

    def cast_weights(self):
        order = ["g01", "u01", "d01", "win", "wuq", "wukv", "wout", "g02", "u02", "d02", "g11", "u11", "d11", "wqkv",
                 "wnout", "g12", "u12", "d12"]
        self.wkeys = {}
        self.wblk = {}
        n = 0
        blocks = [(b * 4, min(4, FC - b * 4)) for b in range(-(-FC // 4))]
        for k in ("g01", "u01"):
            self.wkeys[k] = []
            self.wblk[k] = {}
        for bi, (f0, nf) in enumerate(blocks):
            for k in ("g01", "u01"):
                key = ("wdram", k, "c", bi)
                self.wkeys[k].append(key)
                self.wblk[k][bi] = [key]
                c0, c1 = f0 * 128, (f0 + nf) * 128
                self.S.op("pool", lambda e, k=k, c0=c0, c1=c1: e.dma_start(out=self.wbf[k][:, c0:c1], in_=self.w32[k][:, c0:c1], max_dma_last_dim=2048),
                          writes=[key], dma_sem=f"wc{n}")
                n += 1
            if bi == 1:
                kn = dict(self.S.known["pool"])
                for m in range(n):
                    self.S.eng["pool"].wait_ge(self.S._sem(f"wc{m}"), 16)
                    kn[f"wc{m}"] = 16
                self.S.known["pool"] = kn
        for i, k in enumerate(order):
            if k in ("g01", "u01"):
                continue
            r, c = self.wspec[k]
            nsplit = 2 if k == "d01" else 1
            step = -(-r // nsplit)
            self.wkeys[k] = []
            for r0 in range(0, r, step):
                r1 = min(r, r0 + step)
                key = ("wdram", k, r0)
                self.wkeys[k].append(key)
                self.S.op("pool", lambda e, k=k, r0=r0, r1=r1: e.dma_start(out=self.wbf[k][r0:r1, :], in_=self.w32[k][r0:r1, :], max_dma_last_dim=4096),
                          writes=[key], dma_sem=f"wc{n}")
                n += 1

    def run_job(self, j):
        stages = [
            lambda: self.stage_ffn(j, 0, 1, "x", "h"),
            lambda: self.stage_proj0(j),
            lambda: self.stage_attn0(j),
            lambda: self.stage_outproj(j, "wout", 3),
            lambda: self.stage_ffn(j, 0, 2, "h", "h"),
            lambda: self.stage_ffn(j, 1, 1, "h", "h"),
            lambda: self.stage_proj1(j),
            lambda: self.stage_na(j),
            lambda: self.stage_outproj(j, "wnout", 6 + 3),
            lambda: self.stage_ffn(j, 1, 2, "h", "y"),
        ]
        sel = getattr(self, "stages", None)
        for i, f in enumerate(stages):
            if sel is None or i in sel:
                f()

    def gain(self, idx, c):
        return self.gains[:, idx * 8 + c: idx * 8 + c + 1]

    def hgain(self, idx, c):
        return self.hgains[:, idx * 8 + c: idx * 8 + c + 1]

    def rstd_from_psum(self, ps_ap, npart, T, inv_n, tmp, out, tag):
        S = self.S
        S.op("act", lambda e: e.activation(out=tmp[0:npart, 0:T], in_=ps_ap, func=AF.Sqrt, bias=self.epsc[0:npart, :], scale=inv_n),
             reads=[tag + "_ps", "epsc"], writes=[tag + "_tmp"])
        S.op("dve", lambda e: e.reciprocal(out=out[0:npart, 0:T], in_=tmp[0:npart, 0:T]), reads=[tag + "_tmp"], writes=[tag + "_rstd"])

    def stage_ffn(self, j, l, w, in_mode, out_mode):
        S = self.S
        psb = self.psb
        self.stage_begin()
        J = self.jobs[j]
        first = (l == 0 and w == 1)
        if first and j == "S":
            grp = split_groups(J["nq"]) + [(J["nq"] + s, z) for s, z in split_groups(J["L"] - J["nq"])]
        elif first:
            grp = split_groups(J["L"])
        else:
            grp = split_groups(J["nq"])
        H = self.scr[j]["H"]
        wk = f"{l}{w}"
        Wg, Wu, Wd = self.wbf["g" + wk], self.wbf["u" + wk], self.wbf["d" + wk]
        gi_pre = l * 6 + (0 if w == 1 else 4)
        gi_post = gi_pre + 1
        blocks = [(b * 4, min(4, FC - b * 4)) for b in range(-(-FC // 4))]
        nblk = len(blocks)
        with ExitStack() as st:
            hT = [self.sb(st, f"hT{i}", [128, DC, TG], F32) for i in range(2)]
            hn = [self.sb(st, f"hn{i}", [128, DC, TG], BF16) for i in range(2)]
            sq = self.sb(st, "sq", [128, DC, TG], BF16)
            act = self.sb(st, "act", [128, FC, TG], BF16)
            ysb = self.sb(st, "ysb", [128, DC, TG], F32)
            sg = [self.sb(st, f"sg{i}", [128, TG], F32) for i in range(2)]
            tmpn = self.sb(st, "tmpn", [128, TG], F32)
            rstd = self.sb(st, "rstd", [128, TG], F32)
            rstd2 = rstd
            tmp2 = tmpn
            NW = 3
            wg = [self.sb(st, f"wg{i}", [128, DC, 512], BF16) for i in range(NW)]
            wu = [self.sb(st, f"wu{i}", [128, DC, 512], BF16) for i in range(NW)]
            wdr = self.sb(st, "wdr", [128, FC, D], BF16)
            if out_mode == "y":
                yout = [self.sb(st, f"yout{i}", [128, D], F32) for i in range(2)]
            wlist = [(g, bi) for g in range(len(grp)) for bi in range(nblk)]

            def issue_w(n):
                if n >= len(wlist):
                    return
                g, bi = wlist[n]
                f0, nf = blocks[bi]
                ws = n % NW
                c0, c1 = f0 * 128, (f0 + nf) * 128
                kg_ = self.wblk.get("g" + wk, {}).get(bi, self.wkeys["g" + wk])
                ku_ = self.wblk.get("u" + wk, {}).get(bi, self.wkeys["u" + wk])
                self.dma(wg[ws][:, :, 0:c1 - c0], Wg[:, c0:c1].rearrange("(k p) f -> p k f", p=128), reads=kg_,
                         writes=[f"wg{ws}"], sem=f"wg{ws}")
                self.dma(wu[ws][:, :, 0:c1 - c0], Wu[:, c0:c1].rearrange("(k p) f -> p k f", p=128), reads=ku_,
                         writes=[f"wu{ws}"], sem=f"wu{ws}")

            def issue_wd():
                for bi, (f0, nf) in enumerate(blocks):
                    self.dma(wdr[:, f0:f0 + nf, :], Wd[f0 * 128:(f0 + nf) * 128, :].rearrange("(k p) m -> p k m", p=128),
                             reads=self.wkeys["d" + wk], writes=[("wdr", bi)], sem=f"wdr{bi}")

            def prologue(g):
                t0, T = grp[g]
                s = g % 2
                h = hT[s]
                hk = f"hT{s}"
                if in_mode == "x":
                    tl = tiles128(T)
                    for ti, (a, n) in enumerate(tl):
                        self.dma(ysb[0:n, 2 * ti:2 * ti + 2, :], self.x[j][t0 + a:t0 + a + n, :].rearrange("t (c f) -> t c f", c=2),
                                 writes=[("ysb", 2 * ti), ("ysb", 2 * ti + 1)], sem=f"xin{ti}")
                    for c in range(DC):
                        for ti, (a, n) in enumerate(tl):
                            src = ysb[0:n, 2 * ti + c // 4, (c % 4) * 128:(c % 4 + 1) * 128]
                            S.op("pe", lambda e, a=a, n=n, src=src: e.transpose(psb[7][:, a:a + n], src, self.ident[0:n, 0:n]),
                                 reads=[("ysb", 2 * ti), ("ysb", 2 * ti + 1), "ident"], writes=["ps7"])
                        if c % 2 == 0:
                            S.op("act", lambda e, c=c: e.activation(out=h[:, c, 0:T], in_=psb[7][:, 0:T], func=AF.Copy), reads=["ps7"], writes=[(hk, c)])
                        else:
                            S.op("dve", lambda e, c=c: e.tensor_copy(out=h[:, c, 0:T], in_=psb[7][:, 0:T]), reads=["ps7"], writes=[(hk, c)])
                else:
                    self.dma(h[:, :, 0:T], H[:, t0:t0 + T].rearrange("(c p) t -> p c t", p=128), writes=[(hk, c) for c in range(DC)], sem=f"hload{s}")
                self.prenorm(h, hk, hn[s], f"hn{s}", T, gi_pre, sq, tmpn, rstd, "pre")

            def gateup(g, n):
                t0, T = grp[g]
                s = g % 2
                _, bi = wlist[n]
                f0, nf = blocks[bi]
                ws = n % NW
                for fi in range(nf):
                    fc = f0 + fi
                    pg, pu = psb[fc % 2], psb[2 + fc % 2]
                    kg, ku = f"ps{fc % 2}", f"ps{2 + fc % 2}"
                    for kc in range(DC):
                        S.op("pe", lambda e, kc=kc, fi=fi, pg=pg: e.matmul(pg[:, 0:T], wg[ws][:, kc, fi * 128:(fi + 1) * 128], hn[s][:, kc, 0:T], start=(kc == 0), stop=(kc == DC - 1)),
                             reads=[f"wg{ws}", f"hn{s}"], writes=[kg])
                    for kc in range(DC):
                        S.op("pe", lambda e, kc=kc, fi=fi, pu=pu: e.matmul(pu[:, 0:T], wu[ws][:, kc, fi * 128:(fi + 1) * 128], hn[s][:, kc, 0:T], start=(kc == 0), stop=(kc == DC - 1)),
                             reads=[f"wu{ws}", f"hn{s}"], writes=[ku])
                    sgt = sg[fc % 2]
                    S.op("act", lambda e, pg=pg, sgt=sgt: e.activation(out=sgt[:, 0:T], in_=pg[:, 0:T], func=AF.Silu), reads=[kg], writes=[f"sg{fc % 2}"])
                    S.op("dve", lambda e, pu=pu, sgt=sgt, fc=fc: e.tensor_tensor(out=act[:, fc, 0:T], in0=sgt[:, 0:T], in1=pu[:, 0:T], op=ALU.mult),
                         reads=[ku, f"sg{fc % 2}"], writes=[("act", fc)])

            def down(g):
                t0, T = grp[g]
                for mc in range(DC):
                    py = psb[4 + mc % 2]
                    ky = f"ps{4 + mc % 2}"
                    for fc in range(FC):
                        S.op("pe", lambda e, fc=fc, mc=mc, py=py: e.matmul(py[:, 0:T], wdr[:, fc, mc * 128:(mc + 1) * 128], act[:, fc, 0:T], start=(fc == 0), stop=(fc == FC - 1)),
                             reads=[("wdr", fc // 4), ("act", fc)], writes=[ky])
                    S.op("act", lambda e, mc=mc, py=py: e.activation(out=ysb[:, mc, 0:T], in_=py[:, 0:T], func=AF.Copy), reads=[ky], writes=[("ysb", mc)])
                    S.op("pool", lambda e, mc=mc: e.tensor_tensor(out=sq[:, mc, 0:T], in0=ysb[:, mc, 0:T], in1=ysb[:, mc, 0:T], op=ALU.mult),
                         reads=[("ysb", mc)], writes=[("sq", mc)])

            def epilogue(g):
                t0, T = grp[g]
                s = g % 2
                h = hT[s]
                hk = f"hT{s}"
                self.postnorm_residual(h, hk, ysb, sq, T, gi_post, True, tmp2, rstd2)
                if out_mode == "h":
                    self.dma(H[:, t0:t0 + T].rearrange("(c p) t -> p c t", p=128), h[:, :, 0:T], reads=[(hk, c) for c in range(DC)], sem=f"hstore{s}")
                else:
                    for ti, (a, n) in enumerate(tiles128(T)):
                        yo = yout[ti % 2]
                        yk = f"yout{ti % 2}"
                        for half in range(2):
                            pb = psb[half]
                            for cc in range(4):
                                c = half * 4 + cc
                                S.op("pe", lambda e, c=c, cc=cc, a=a, n=n, pb=pb: e.transpose(pb[0:n, cc * 128:(cc + 1) * 128], h[:, c, a:a + n], self.ident[:, :]),
                                     reads=[(hk, c), "ident"], writes=[f"ps{half}"])
                            if half == 0:
                                S.op("act", lambda e, n=n, pb=pb, yo=yo: e.activation(out=yo[0:n, 0:512], in_=pb[0:n, :], func=AF.Copy), reads=["ps0"], writes=[yk + "a"])
                            else:
                                S.op("dve", lambda e, n=n, pb=pb, yo=yo: e.tensor_copy(out=yo[0:n, 512:1024], in_=pb[0:n, :]), reads=["ps1"], writes=[yk + "b"])
                        tok0 = t0 + a
                        lo = max(tok0, NMETA)
                        hi = tok0 + n
                        if hi > lo:
                            self.dma(self.y[j][lo - NMETA:hi - NMETA, :], yo[lo - tok0:n, :], reads=[yk + "a", yk + "b"], sem=f"yst{ti % 2}")

            issue_w(0)
            prologue(0)
            issue_w(1)
            issue_wd()
            n = 0
            for g in range(len(grp)):
                for bi in range(nblk):
                    issue_w(n + 2)
                    gateup(g, n)
                    n += 1
                    if bi == nblk - 2 and g + 1 < len(grp):
                        prologue(g + 1)
                down(g)
                if g + 1 < len(grp):
                    issue_wd()
                epilogue(g)

    def rstd_act(self, ps_ap, psk, tmp_ap, tmpk, rstd_ap, rstdk, inv_n, npart=128):
        S = self.S
        S.op("act", lambda e: e.activation(out=tmp_ap, in_=ps_ap, func=AF.Ln, bias=self.epsc[0:npart, :], scale=inv_n),
             reads=[psk, "epsc"], writes=[tmpk])
        S.op("act", lambda e: e.activation(out=rstd_ap, in_=tmp_ap, func=AF.Exp, scale=-0.5), reads=[tmpk], writes=[rstdk])

    def prenorm(self, h, hk, out, outk, T, gi, sq, tmp, rstd, tag, nch=DC, inv_n=1.0 / D, gain_fn=None, ps=6):
        S = self.S
        psb = self.psb
        if gain_fn is None:
            gain_fn = lambda c: self.gain(gi, c)
        for c in range(nch):
            S.op("pool", lambda e, c=c: e.tensor_tensor(out=sq[:, c, 0:T], in0=h[:, c, 0:T], in1=h[:, c, 0:T], op=ALU.mult),
                 reads=[(hk, c)], writes=[("sq", c)])
        for c in range(nch):
            S.op("pe", lambda e, c=c: e.matmul(psb[ps][:, 0:T], self.onesb[:, :], sq[:, c, 0:T], start=(c == 0), stop=(c == nch - 1)),
                 reads=[("sq", c), "onesb"], writes=[f"ps{ps}"])
        self.rstd_act(psb[ps][:, 0:T], f"ps{ps}", tmp[:, 0:T], tag + "t", rstd[:, 0:T], tag + "r", inv_n)
        for c in range(nch):
            S.op("dve", lambda e, c=c: e.scalar_tensor_tensor(out=out[:, c, 0:T], in0=h[:, c, 0:T], scalar=gain_fn(c), in1=rstd[:, 0:T], op0=ALU.mult, op1=ALU.mult),
                 reads=[(hk, c), tag + "r", "gains", "small"], writes=[outk])

    def postnorm_residual(self, h, hk, ysb, sq, T, gi, half, tmp, rstd):
        S = self.S
        psb = self.psb
        for c in range(DC):
            S.op("pe", lambda e, c=c: e.matmul(psb[7][:, 0:T], self.onesb[:, :], sq[:, c, 0:T], start=(c == 0), stop=(c == DC - 1)),
                 reads=[("sq", c), "onesb"], writes=["ps7"])
        self.rstd_act(psb[7][:, 0:T], "ps7", tmp[:, 0:T], "pret", rstd[:, 0:T], "prer", 1.0 / D)
        for c in range(DC):
            gap = self.hgain(gi, c) if half else self.gain(gi, c)
            S.op("dve", lambda e, c=c: e.tensor_tensor(out=ysb[:, c, 0:T], in0=ysb[:, c, 0:T], in1=rstd[:, 0:T], op=ALU.mult),
                 reads=[("ysb", c), "prer"], writes=[("ysb", c)])
            S.op("dve", lambda e, c=c, gap=gap: e.scalar_tensor_tensor(out=h[:, c, 0:T], in0=ysb[:, c, 0:T], scalar=gap, in1=h[:, c, 0:T], op0=ALU.mult, op1=ALU.add),
                 reads=[("ysb", c), (hk, c), "gains", "hgains"], writes=[(hk, c)])

    def stage_outproj(self, j, wname, gi):
        S = self.S
        psb = self.psb
        self.stage_begin()
        J = self.jobs[j]
        grp = split_groups(J["nq"])
        H = self.scr[j]["H"]
        O = self.scr[j]["O"]
        with ExitStack() as st:
            hT = [self.sb(st, f"hT{i}", [128, DC, TG], F32) for i in range(2)]
            ob = [self.sb(st, f"ob{i}", [128, DC, TG], BF16) for i in range(2)]
            sq = self.sb(st, "sq", [128, DC, TG], BF16)
            ysb = self.sb(st, "ysb", [128, DC, TG], F32)
            rstd2 = self.sb(st, "rstd2", [128, TG], F32)
            tmp2 = self.sb(st, "tmp2", [128, TG], F32)
            wo = self.sb(st, "wo", [128, DC, D], BF16)
            self.dma(wo[:], self.wbf[wname].rearrange("(k p) m -> p k m", p=128), reads=self.wkeys[wname], writes=["wo"], sem="wo")
            def loads(g):
                t0, T = grp[g]
                s = g % 2
                self.dma(hT[s][:, :, 0:T], H[:, t0:t0 + T].rearrange("(c p) t -> p c t", p=128), writes=[(f"hT{s}", c) for c in range(DC)], sem=f"hload{s}")
                self.dma(ob[s][:, :, 0:T], O[:, t0:t0 + T].rearrange("(c p) t -> p c t", p=128), writes=[f"ob{s}"], sem=f"oload{s}")

            loads(0)
            for g, (t0, T) in enumerate(grp):
                s = g % 2
                h = hT[s]
                hk = f"hT{s}"
                if g + 1 < len(grp):
                    loads(g + 1)
                for mc in range(DC):
                    py = psb[mc % 4]
                    ky = f"ps{mc % 4}"
                    for kc in range(DC):
                        S.op("pe", lambda e, kc=kc, mc=mc, py=py: e.matmul(py[:, 0:T], wo[:, kc, mc * 128:(mc + 1) * 128], ob[s][:, kc, 0:T], start=(kc == 0), stop=(kc == DC - 1)),
                             reads=["wo", f"ob{s}"], writes=[ky])
                    S.op("act", lambda e, mc=mc, py=py: e.activation(out=ysb[:, mc, 0:T], in_=py[:, 0:T], func=AF.Copy), reads=[ky], writes=[("ysb", mc)])
                    S.op("pool", lambda e, mc=mc: e.tensor_tensor(out=sq[:, mc, 0:T], in0=ysb[:, mc, 0:T], in1=ysb[:, mc, 0:T], op=ALU.mult),
                         reads=[("ysb", mc)], writes=[("sq", mc)])
                self.postnorm_residual(h, hk, ysb, sq, T, gi, False, tmp2, rstd2)
                self.dma(H[:, t0:t0 + T].rearrange("(c p) t -> p c t", p=128), h[:, :, 0:T], reads=[(hk, c) for c in range(DC)], sem=f"hstore{s}")

    def stage_proj0(self, j):
        S = self.S
        psb = self.psb
        self.stage_begin()
        J = self.jobs[j]
        nq, L = J["nq"], J["L"]
        grp = [(s, z, True) for s, z in split_groups(nq)]
        if L > nq:
            grp += [(nq + s, z, False) for s, z in split_groups(L - nq)]
        sc = self.scr[j]
        H = sc["H"]
        with ExitStack() as st:
            hT = [self.sb(st, f"hT{i}", [128, DC, TG], F32) for i in range(2)]
            ab = [self.sb(st, f"ab{i}", [128, DC, TG], BF16) for i in range(2)]
            sq = self.sb(st, "sq", [128, DC, TG], BF16)
            tmpn = self.sb(st, "tmpn", [128, TG], F32)
            rstd = self.sb(st, "rstd", [128, TG], F32)
            win = self.sb(st, "win", [128, DC, IN_COLS], BF16)
            wuq = self.sb(st, "wuq", [128, 2, 768], BF16)
            tabs = [[self.sb(st, f"tab{i}_{k}", [128, TG], F32) for k in range(4)] for i in range(2)]
            cq32 = self.sb(st, "cq32", [128, 2, TG], F32)
            cqn = self.sb(st, "cqn", [128, 2, TG], BF16)
            e32 = [self.sb(st, f"e32_{i}", [128, TG], F32) for i in range(2)]
            n32 = [self.sb(st, f"n32_{i}", [128, TG], F32) for i in range(2)]
            t1 = [self.sb(st, f"t1_{i}", [128, TG], F32) for i in range(2)]
            t2 = [self.sb(st, f"t2_{i}", [128, TG], F32) for i in range(2)]
            sq1 = [self.sb(st, f"sq1_{i}", [128, TG], BF16) for i in range(2)]
            tm1 = [self.sb(st, f"tm1_{i}", [128, TG], F32) for i in range(2)]
            rs1 = [self.sb(st, f"rs1_{i}", [128, TG], F32) for i in range(2)]
            qst = [self.sb(st, f"qst{i}", [128, 6, TG], BF16) for i in range(2)]
            gst = [self.sb(st, f"gst{i}", [128, 4, TG], BF16) for i in range(2)]
            kst = [self.sb(st, f"kst{i}", [128, 3, TG], BF16) for i in range(2)]
            vst = [self.sb(st, f"vst{i}", [128, 4, 128], BF16) for i in range(2)]
            self.dma(win[:], self.wbf["win"].rearrange("(k p) m -> p k m", p=128), reads=self.wkeys["win"], writes=["win"], sem="win")
            self.dma(wuq[:], self.wbf["wuq"].rearrange("(k p) m -> p k m", p=128), reads=self.wkeys["wuq"], writes=["wuq"], sem="wuq")
            cnt = [0]

            def rope_a(src32, srck, npart, T, cos, tabk, ps, u):
                S.op("pe", lambda e: e.matmul(psb[ps][0:npart, 0:T], self.rm[0:npart, 0:npart], src32[0:npart, 0:T], start=True, stop=True),
                     reads=[srck, "rm"], writes=[f"ps{ps}"])
                S.op("pool", lambda e: e.tensor_tensor(out=t1[u][0:npart, 0:T], in0=src32[0:npart, 0:T], in1=cos[0:npart, 0:T], op=ALU.mult),
                     reads=[srck, tabk], writes=[f"t1_{u}"])

            def rope_b(npart, T, sin, tabk, out_ap, outk, ps, u):
                sink = tabk.replace("cos", "sin")
                S.op("dve", lambda e: e.tensor_tensor(out=t2[u][0:npart, 0:T], in0=sin[0:npart, 0:T], in1=psb[ps][0:npart, 0:T], op=ALU.mult),
                     reads=[f"ps{ps}", sink], writes=[f"t2_{u}"])
                S.op("dve", lambda e: e.tensor_tensor(out=out_ap, in0=t1[u][0:npart, 0:T], in1=t2[u][0:npart, 0:T], op=ALU.add),
                     reads=[f"t1_{u}", f"t2_{u}"], writes=[outk])

            def rope(src32, srck, npart, T, cos, sin, tabk, out_ap, outk, ps, u):
                rope_a(src32, srck, npart, T, cos, tabk, ps, u)
                rope_b(npart, T, sin, tabk, out_ap, outk, ps, u)

            def proj(col0, ncol, T, s, ps):
                for kc in range(DC):
                    S.op("pe", lambda e, kc=kc: e.matmul(psb[ps][0:ncol, 0:T], win[:, kc, col0:col0 + ncol], ab[s][:, kc, 0:T], start=(kc == 0), stop=(kc == DC - 1)),
                         reads=["win", f"ab{s}"], writes=[f"ps{ps}"])

            def norm_p1a(col0, T, s, u):
                proj(col0, 128, T, s, 4 + u)
                S.op("act", lambda e: e.activation(out=e32[u][:, 0:T], in_=psb[4 + u][:, 0:T], func=AF.Copy), reads=[f"ps{4 + u}"], writes=[f"e32_{u}"])
                S.op("pool", lambda e: e.tensor_tensor(out=sq1[u][:, 0:T], in0=e32[u][:, 0:T], in1=e32[u][:, 0:T], op=ALU.mult),
                     reads=[f"e32_{u}"], writes=[f"sq1_{u}"])

            def norm_p1b(T, u, ones_ap, onesk, inv_n):
                S.op("pe", lambda e: e.matmul(psb[6 + u][:, 0:T], ones_ap, sq1[u][:, 0:T], start=True, stop=True), reads=[f"sq1_{u}", onesk], writes=[f"ps{6 + u}"])
                self.rstd_act(psb[6 + u][:, 0:T], f"ps{6 + u}", tm1[u][:, 0:T], f"tm1_{u}", rs1[u][:, 0:T], f"rs1_{u}", inv_n)

            def norm_p2(T, u, gcol):
                S.op("dve", lambda e: e.scalar_tensor_tensor(out=n32[u][:, 0:T], in0=e32[u][:, 0:T], scalar=self.small[:, gcol:gcol + 1], in1=rs1[u][:, 0:T], op0=ALU.mult, op1=ALU.mult),
                     reads=[f"e32_{u}", f"rs1_{u}", "small"], writes=[f"n32_{u}"])

            def loads(g):
                t0, T, _ = grp[g]
                s = g % 2
                self.dma(hT[s][:, :, 0:T], H[:, t0:t0 + T].rearrange("(c p) t -> p c t", p=128), writes=[(f"hT{s}", c) for c in range(DC)], sem=f"hload{s}")
                for k, nm in enumerate(("cosM", "sinM", "cosA", "sinA")):
                    self.dma(tabs[s][k][:, 0:T], self.tab[(j, nm)][:, t0:t0 + T], writes=[f"tab{s}" + nm], sem=f"tab{s}_{k}")

            nop = lambda: None
            loads(0)
            for g, (t0, T, full) in enumerate(grp):
                s = g % 2
                h = hT[s]
                hk = f"hT{s}"
                tb = tabs[s]
                tabk = f"tab{s}"
                if g + 1 < len(grp):
                    loads(g + 1)
                self.prenorm(h, hk, ab[s], f"ab{s}", T, 2, sq, tmpn, rstd, "pre")
                items = []
                if full:
                    def cq_p1(T=T, s=s):
                        for c in range(2):
                            proj(c * 128, 128, T, s, c)
                            S.op("act", lambda e, c=c: e.activation(out=cq32[:, c, 0:T], in_=psb[c][:, 0:T], func=AF.Copy), reads=[f"ps{c}"], writes=[("cq32", c)])
                        self.prenorm(cq32, "cq32", cqn, "cqn", T, 0, sq, tmpn, rstd, "pre", nch=2, inv_n=1.0 / 256,
                                     gain_fn=lambda c: self.small[:, c:c + 1], ps=3)

                    def cq_p2(T=T, s=s, tb=tb, tabk=tabk, t0=t0):
                        for mc in range(6):
                            ps = mc % 2
                            for kc in range(2):
                                S.op("pe", lambda e, kc=kc, mc=mc, ps=ps: e.matmul(psb[ps][:, 0:T], wuq[:, kc, mc * 128:(mc + 1) * 128], cqn[:, kc, 0:T], start=(kc == 0), stop=(kc == 1)),
                                     reads=["wuq", "cqn"], writes=[f"ps{ps}"])
                            if mc < 4:
                                S.op("act", lambda e, mc=mc, ps=ps: e.activation(out=qst[s][:, mc, 0:T], in_=psb[ps][:, 0:T], func=AF.Copy), reads=[f"ps{ps}"], writes=[(f"qst{s}", mc)])
                            else:
                                u = mc % 2
                                S.op("act", lambda e, ps=ps, u=u: e.activation(out=cq32[:, u, 0:T], in_=psb[ps][:, 0:T], func=AF.Copy), reads=[f"ps{ps}"], writes=[("cq32", u)])
                                rope(cq32[:, u, :], ("cq32", u), 128, T, tb[0], tb[1], tabk + "cosM", qst[s][:, mc, 0:T], (f"qst{s}", mc), 2 + u, u)
                        self.dma(sc["QA"][:, t0:t0 + T].rearrange("(c p) t -> p c t", p=128), qst[s][:, :, 0:T], reads=[(f"qst{s}", mc) for mc in range(6)], sem=f"qa{s}")
                    items.append((cq_p1, nop, cq_p2, nop))
                    for c in range(4):
                        u = c % 2

                        def q_p1a(c=c, T=T, s=s, u=u):
                            norm_p1a(416 + c * 128, T, s, u)

                        def q_p1b(T=T, u=u):
                            norm_p1b(T, u, self.blkones[:, :], "blkones", 1.0 / 64)

                        def q_p2a(T=T, u=u, tb=tb, tabk=tabk):
                            norm_p2(T, u, 3)
                            rope_a(n32[u], f"n32_{u}", 128, T, tb[2], tabk + "cosA", 2 + u, u)

                        def q_p2b(c=c, T=T, s=s, u=u, tb=tb, tabk=tabk, t0=t0):
                            rope_b(128, T, tb[3], tabk + "cosA", gst[s][:, c, 0:T], (f"gst{s}", c), 2 + u, u)
                            if c == 3:
                                self.dma(sc["QG"][:, t0:t0 + T].rearrange("(c p) t -> p c t", p=128), gst[s][:, :, 0:T], reads=[(f"gst{s}", cc) for cc in range(4)], sem=f"qg{s}")
                        items.append((q_p1a, q_p1b, q_p2a, q_p2b))

                def k_p1a(T=T, s=s):
                    norm_p1a(928, T, s, 0)

                def k_p1b(T=T):
                    norm_p1b(T, 0, self.blkones[:, :], "blkones", 1.0 / 64)

                def k_p2a(T=T, tb=tb, tabk=tabk):
                    norm_p2(T, 0, 4)
                    rope_a(n32[0], "n32_0", 128, T, tb[2], tabk + "cosA", 2, 0)

                def k_p2b(T=T, s=s, tb=tb, tabk=tabk, t0=t0):
                    rope_b(128, T, tb[3], tabk + "cosA", kst[s][:, 2, 0:T], (f"kst{s}", 2), 2, 0)
                    self.dma(sc["KG"][:, t0:t0 + T], kst[s][:, 2, 0:T], reads=[(f"kst{s}", 2)], sem=f"kg{s}")
                items.append((k_p1a, k_p1b, k_p2a, k_p2b))

                def kv_p1a(T=T, s=s):
                    norm_p1a(256, T, s, 1)

                def kv_p1b(T=T):
                    norm_p1b(T, 1, self.onesb[:, :], "onesb", 1.0 / 128)

                def kv_p2(T=T, s=s, t0=t0):
                    S.op("dve", lambda e: e.scalar_tensor_tensor(out=kst[s][:, 0, 0:T], in0=e32[1][:, 0:T], scalar=self.small[:, 2:3], in1=rs1[1][:, 0:T], op0=ALU.mult, op1=ALU.mult),
                         reads=["e32_1", "rs1_1", "small"], writes=[(f"kst{s}", 0)])
                    self.dma(sc["CKV"][:, t0:t0 + T], kst[s][:, 0, 0:T], reads=[(f"kst{s}", 0)], sem=f"ckv{s}")
                items.append((kv_p1a, kv_p1b, kv_p2, nop))

                def kr_p1(T=T, s=s):
                    proj(384, 32, T, s, 4)
                    S.op("act", lambda e: e.activation(out=e32[0][0:32, 0:T], in_=psb[4][0:32, 0:T], func=AF.Copy), reads=["ps4"], writes=["e32_0"])

                def kr_p2a(T=T, tb=tb, tabk=tabk):
                    rope_a(e32[0], "e32_0", 32, T, tb[0], tabk + "cosM", 2, 0)

                def kr_p2b(T=T, s=s, tb=tb, tabk=tabk, t0=t0):
                    rope_b(32, T, tb[1], tabk + "cosM", kst[s][0:32, 1, 0:T], (f"kst{s}", 1), 2, 0)
                    self.dma(sc["KR"][:, t0:t0 + T], kst[s][0:32, 1, 0:T], reads=[(f"kst{s}", 1)], sem=f"kr{s}")
                items.append((kr_p1, nop, kr_p2a, kr_p2b))

                def v_p1(T=T, s=s, t0=t0):
                    for ti, (a, n) in enumerate(tiles128(T)):
                        for kc in range(DC):
                            S.op("pe", lambda e, kc=kc, a=a, n=n, ti=ti: e.matmul(psb[5][0:n, ti * 128:(ti + 1) * 128], ab[s][:, kc, a:a + n], win[:, kc, 1056:1184], start=(kc == 0), stop=(kc == DC - 1)),
                                 reads=["win", f"ab{s}"], writes=["ps5"])
                        S.op("act", lambda e, n=n, ti=ti: e.activation(out=vst[s][0:n, ti, :], in_=psb[5][0:n, ti * 128:(ti + 1) * 128], func=AF.Copy), reads=["ps5"], writes=[(f"vst{s}", ti)])
                        self.dma(sc["VG"][t0 + a:t0 + a + n, :], vst[s][0:n, ti, :], reads=[(f"vst{s}", ti)], sem=f"vg{s}_{ti}")
                items.append((v_p1, nop, nop, nop))
                for i in range(len(items) + 1):
                    if i < len(items):
                        items[i][0]()
                    if i >= 1:
                        items[i - 1][2]()
                    if i < len(items):
                        items[i][1]()
                    if i >= 1:
                        items[i - 1][3]()

    def normalize(self, T, ps_o, oslot, osb, rden, dst):
        S = self.S
        psb = self.psb
        S.op("dve", lambda e: e.reciprocal(out=rden[0:64, 0:T], in_=psb[ps_o][64:128, 0:T]), reads=[f"ps{ps_o}"], writes=["rden"])
        S.op("dve", lambda e: e.tensor_tensor(out=osb[1 + oslot][0:64, 0:T], in0=psb[ps_o][0:64, 0:T], in1=rden[0:64, 0:T], op=ALU.mult),
             reads=[f"ps{ps_o}", "rden"], writes=[f"osbb{oslot}"])
        self.dma(dst, osb[1 + oslot][0:64, 0:T], reads=[f"osbb{oslot}"], sem=f"ost{oslot}")

    def stage_attn0(self, j):
        S = self.S
        psb = self.psb
        self.stage_begin()
        J = self.jobs[j]
        nq, L = J["nq"], J["L"]
        sc = self.scr[j]
        qgrp = split_groups(nq)
        kch = tiles128(L)
        nkc = len(kch)
        with ExitStack() as st:
            ckv = self.sb(st, "ckv", [128, L], BF16)
            wukv = self.sb(st, "wukv", [128, 1024], BF16)
            Kb = [self.sb(st, f"Kb{i}", [128, L], BF16) for i in range(2)]
            Vx = [self.sb(st, f"Vx{i}", [128, nkc, 128], BF16) for i in range(2)]
            Qb = [self.sb(st, f"Qb{i}", [128, nq], BF16) for i in range(2)]
            pts = [self.sb(st, f"pt{i}", [128, 2, TG], BF16) for i in range(3)]
            osb32 = self.sb(st, "osb32", [64, TG], F32)
            osbb = [self.sb(st, f"osbb{i}", [64, TG], BF16) for i in range(2)]
            rden = self.sb(st, "rden", [128, TG], F32)
            osb = [osb32] + osbb
            self.dma(ckv[:], sc["CKV"], writes=["ckv"], sem="ckv")
            self.dma(wukv[:], self.wbf["wukv"], reads=self.wkeys["wukv"], writes=["wukv"], sem="wukv")
            for i in range(2):
                S.op("pool", lambda e, i=i: e.memset(Kb[i][64:128, :], 0.0), writes=[f"Kr{i}"])
                self.dma(Kb[i][64:96, :], sc["KR"], writes=[f"Kr{i}"], sem=f"kr{i}")
                S.op("pool", lambda e, i=i: e.memset(Vx[i][:, :, 64:128], 1.0), writes=[f"Vone{i}"])
            units = [("m", h, h, True) for h in range(8)] + [("g", qh // 4, qh, qh % 4 == 0) for qh in range(8)]
            kslot = []
            kvs = -1
            for (kind, kvh, qh, newkv) in units:
                if newkv:
                    kvs += 1
                kslot.append(kvs % 2)

            def prep_steps(ui):
                kind, kvh, qh, newkv = units[ui]
                qs = ui % 2
                ks = kslot[ui]
                if kind == "m":
                    self.dma(Qb[qs][0:64, :], sc["QA"][qh * 64:(qh + 1) * 64, :], writes=[f"Qn{qs}"], sem=f"qn{qs}")
                    self.dma(Qb[qs][64:96, :], sc["QA"][512 + qh * 32:512 + (qh + 1) * 32, :], writes=[f"Qr{qs}"], sem=f"qr{qs}")
                else:
                    if qh < 2:
                        S.op("pool", lambda e: e.memset(Qb[qs][64:128, :], 0.0), writes=[f"Qr{qs}"])
                    self.dma(Qb[qs][0:64, :], sc["QG"][qh * 64:(qh + 1) * 64, :], writes=[f"Qn{qs}"], sem=f"qn{qs}")
                if newkv and kind == "g":
                    self.dma(Kb[ks][0:64, :], sc["KG"][kvh * 64:(kvh + 1) * 64, :], writes=[f"Kn{ks}"], sem=f"kg{ks}")
                    nfull = L // 128
                    if nfull:
                        self.dma(Vx[ks][:, 0:nfull, 0:64], sc["VG"][0:nfull * 128, kvh * 64:(kvh + 1) * 64].rearrange("(c p) f -> p c f", p=128),
                                 writes=[f"Vv{ks}"], sem=f"vg{ks}")
                    if L % 128:
                        n = L % 128
                        self.dma(Vx[ks][0:n, nfull, 0:64], sc["VG"][nfull * 128:L, kvh * 64:(kvh + 1) * 64], writes=[f"Vt{ks}"], sem=f"vgt{ks}")
                yield
                if newkv and kind == "m":
                    for bi, (a, n) in enumerate([(s0, min(512, L - s0)) for s0 in range(0, L, 512)]):
                        pb = (4, 5)[bi % 2]
                        S.op("pe", lambda e, a=a, n=n, pb=pb: e.matmul(psb[pb][0:64, 0:n], wukv[:, kvh * 128:kvh * 128 + 64], ckv[:, a:a + n], start=True, stop=True),
                             reads=["wukv", "ckv"], writes=[f"ps{pb}"])
                        S.op("dve", lambda e, a=a, n=n, pb=pb: e.tensor_copy(out=Kb[ks][0:64, a:a + n], in_=psb[pb][0:64, 0:n]), reads=[f"ps{pb}"], writes=[f"Kn{ks}"])
                        yield
                    for c0 in range(0, nkc, 8):
                        cs = list(range(c0, min(nkc, c0 + 8)))
                        pb = (4, 5)[(c0 // 8) % 2]
                        for ci in cs:
                            a, n = kch[ci]
                            S.op("pe", lambda e, a=a, n=n, ci=ci, pb=pb: e.matmul(psb[pb][0:n, (ci - c0) * 64:(ci - c0 + 1) * 64], ckv[:, a:a + n], wukv[:, kvh * 128 + 64:kvh * 128 + 128], start=True, stop=True),
                                 reads=["wukv", "ckv"], writes=[f"ps{pb}"])
                        full = [ci for ci in cs if kch[ci][1] == 128]
                        if full:
                            nf = len(full)
                            S.op("dve", lambda e, c0=c0, nf=nf, pb=pb: e.tensor_copy(out=Vx[ks][:, c0:c0 + nf, 0:64], in_=psb[pb][:, 0:nf * 64].rearrange("p (c f) -> p c f", f=64)),
                                 reads=[f"ps{pb}"], writes=[f"Vv{ks}"])
                        for ci in cs:
                            a, n = kch[ci]
                            if n < 128:
                                S.op("dve", lambda e, n=n, ci=ci, pb=pb: e.tensor_copy(out=Vx[ks][0:n, ci, 0:64], in_=psb[pb][0:n, (ci - c0) * 64:(ci - c0 + 1) * 64]),
                                     reads=[f"ps{pb}"], writes=[f"Vv{ks}"])
                        yield

            for _ in prep_steps(0):
                pass
            og = 0
            for ui, (kind, kvh, qh, newkv) in enumerate(units):
                qs = ui % 2
                ks = kslot[ui]
                filler = prep_steps(ui + 1) if ui + 1 < len(units) else iter(())
                if kind == "m":
                    dk, scale, orow = 96, 96.0 ** -0.5, qh * 64
                else:
                    dk, scale, orow = 128, 64.0 ** -0.5, 512 + qh * 64
                for (q0, T) in qgrp:
                    self.attn_core_keys(T, q0, Kb[ks], [f"Kn{ks}", f"Kr{ks}"], dk, L, Qb[qs], [f"Qn{qs}", f"Qr{qs}"], Vx[ks], [f"Vv{ks}", f"Vt{ks}", f"Vone{ks}"],
                                        scale, pts, 6 + og % 2, og % 2, sc["O"][orow:orow + 64, q0:q0 + T], osb, rden, filler)
                    og += 1
                for _ in filler:
                    pass

    def attn_core_keys(self, T, q0, Kb, Kks, dk, nkeys, Qb, Qks, Vx, Vks, scale, pts, ps_o, oslot, dst, osb, rden, filler=None):
        S = self.S
        psb = self.psb
        pp = self.pp
        kch = tiles128(nkeys)
        nk = len(kch)
        pairs = [list(range(i, min(nk, i + 2))) for i in range(0, nk, 2)]
        npair = len(pairs)

        def s_pair(p):
            b = p % 3
            for hi, i in enumerate(pairs[p]):
                a, n = kch[i]
                S.op("pe", lambda e, a=a, n=n, hi=hi: e.matmul(pp[b][0:n, hi, 0:T], Kb[0:dk, a:a + n], Qb[0:dk, q0:q0 + T], start=True, stop=True),
                     reads=Kks + Qks, writes=[f"ps{2 * b + hi}"])
            ns = [kch[i][1] for i in pairs[p]]
            if len(ns) == 2 and ns[0] == ns[1]:
                n = ns[0]
                S.op("act", lambda e: e.activation(out=pts[b][0:n, :, 0:T], in_=pp[b][0:n, :, 0:T], func=AF.Exp, scale=scale), reads=[f"ps{2 * b}", f"ps{2 * b + 1}"], writes=[f"pt{b}"])
            else:
                for hi, n in enumerate(ns):
                    S.op("act", lambda e, hi=hi, n=n: e.activation(out=pts[b][0:n, hi, 0:T], in_=pp[b][0:n, hi, 0:T], func=AF.Exp, scale=scale), reads=[f"ps{2 * b + hi}"], writes=[f"pt{b}"])

        def pv_pair(p):
            b = p % 3
            for hi, i in enumerate(pairs[p]):
                a, n = kch[i]
                S.op("pe", lambda e, n=n, hi=hi, i=i: e.matmul(psb[ps_o][:, 0:T], Vx[0:n, i, :], pts[b][0:n, hi, 0:T], start=(i == 0), stop=(i == nk - 1)),
                     reads=Vks + [f"pt{b}"], writes=[f"ps{ps_o}"])

        for p in range(min(2, npair)):
            s_pair(p)
        for p in range(npair):
            if p + 2 < npair:
                s_pair(p + 2)
            pv_pair(p)
            if filler is not None:
                next(filler, None)
        self.normalize(T, ps_o, oslot, osb, rden, dst)

    def stage_proj1(self, j):
        S = self.S
        psb = self.psb
        self.stage_begin()
        J = self.jobs[j]
        grp = split_groups(J["nq"])
        sc = self.scr[j]
        H = sc["H"]
        with ExitStack() as st:
            hT = [self.sb(st, f"hT{i}", [128, DC, TG], F32) for i in range(2)]
            ab = [self.sb(st, f"ab{i}", [128, DC, TG], BF16) for i in range(2)]
            sq = self.sb(st, "sq", [128, DC, TG], BF16)
            tmpn = self.sb(st, "tmpn", [128, TG], F32)
            rstd = self.sb(st, "rstd", [128, TG], F32)
            wq = self.sb(st, "wq", [128, DC, 3072], BF16)
            qst = [self.sb(st, f"qst{i}", [128, DC, TG], BF16) for i in range(2)]
            kst = [self.sb(st, f"kst{i}", [128, DC, TG], BF16) for i in range(2)]
            vst = [self.sb(st, f"vst{i}", [128, D], BF16) for i in range(2)]
            for k3 in range(3):
                self.dma(wq[:, :, k3 * 1024:(k3 + 1) * 1024], self.wbf["wqkv"][:, k3 * 1024:(k3 + 1) * 1024].rearrange("(k p) m -> p k m", p=128),
                         reads=self.wkeys["wqkv"], writes=[("wq", k3)], sem=f"wq{k3}")
            vi = 0

            def loads(g):
                t0, T = grp[g]
                s = g % 2
                self.dma(hT[s][:, :, 0:T], H[:, t0:t0 + T].rearrange("(c p) t -> p c t", p=128), writes=[(f"hT{s}", c) for c in range(DC)], sem=f"hload{s}")

            loads(0)
            for g, (t0, T) in enumerate(grp):
                s = g % 2
                h = hT[s]
                hk = f"hT{s}"
                if g + 1 < len(grp):
                    loads(g + 1)
                self.prenorm(h, hk, ab[s], f"ab{s}", T, 6 + 2, sq, tmpn, rstd, "pre")
                for mc in range(16):
                    pb = mc % 4
                    for kc in range(DC):
                        S.op("pe", lambda e, kc=kc, mc=mc, pb=pb: e.matmul(psb[pb][:, 0:T], wq[:, kc, mc * 128:(mc + 1) * 128], ab[s][:, kc, 0:T], start=(kc == 0), stop=(kc == DC - 1)),
                             reads=[("wq", mc // 8), f"ab{s}"], writes=[f"ps{pb}"])
                    if mc < 8:
                        S.op("act", lambda e, mc=mc, pb=pb: e.activation(out=qst[s][:, mc, 0:T], in_=psb[pb][:, 0:T], func=AF.Copy, scale=0.125), reads=[f"ps{pb}"], writes=[(f"qst{s}", mc)])
                    else:
                        S.op("dve", lambda e, mc=mc, pb=pb: e.tensor_copy(out=kst[s][:, mc - 8, 0:T], in_=psb[pb][:, 0:T]), reads=[f"ps{pb}"], writes=[(f"kst{s}", mc - 8)])
                self.dma(sc["Q1"][:, t0:t0 + T].rearrange("(c p) t -> p c t", p=128), qst[s][:, :, 0:T], reads=[(f"qst{s}", c) for c in range(DC)], sem=f"q1{s}")
                self.dma(sc["K1"][:, t0:t0 + T].rearrange("(c p) t -> p c t", p=128), kst[s][:, :, 0:T], reads=[(f"kst{s}", c) for c in range(DC)], sem=f"k1{s}")
                for ti, (a, n) in enumerate(tiles128(T)):
                    vs = vi % 2
                    vi += 1
                    for half in range(2):
                        pb = 4 + half
                        for kc in range(DC):
                            S.op("pe", lambda e, kc=kc, a=a, n=n, pb=pb, half=half: e.matmul(psb[pb][0:n, :], ab[s][:, kc, a:a + n], wq[:, kc, 2048 + half * 512:2048 + (half + 1) * 512], start=(kc == 0), stop=(kc == DC - 1)),
                                 reads=[("wq", 2), f"ab{s}"], writes=[f"ps{pb}"])
                        if half == 0:
                            S.op("act", lambda e, n=n, pb=pb, vs=vs: e.activation(out=vst[vs][0:n, 0:512], in_=psb[pb][0:n, :], func=AF.Copy), reads=[f"ps{pb}"], writes=[f"vst{vs}a"])
                        else:
                            S.op("dve", lambda e, n=n, pb=pb, vs=vs: e.tensor_copy(out=vst[vs][0:n, 512:1024], in_=psb[pb][0:n, :]), reads=[f"ps{pb}"], writes=[f"vst{vs}b"])
                    self.dma(sc["V1"][t0 + a:t0 + a + n, :], vst[vs][0:n, :], reads=[f"vst{vs}a", f"vst{vs}b"], sem=f"v1{vs}")

    def stage_na(self, j):
        S = self.S
        psb = self.psb
        self.stage_begin()
        J = self.jobs[j]
        nq, rows = J["nq"], J["rows"]
        sc = self.scr[j]
        nche = rows // 2
        ncho = rows // 2 - 1
        with ExitStack() as st:
            Kb = [self.sb(st, f"Kb{i}", [64, nq], BF16) for i in range(2)]
            Qb = [self.sb(st, f"Qb{i}", [64, nq], BF16) for i in range(2)]
            Vxe = [self.sb(st, f"Vxe{i}", [128, nche, 128], BF16) for i in range(2)]
            Vxo = [self.sb(st, f"Vxo{i}", [128, ncho, 128], BF16) for i in range(2)]
            Vm = [self.sb(st, f"Vm{i}", [16, 128], BF16) for i in range(2)]
            btf = self.sb(st, "btf", [128, 1024], F32)
            btb = [self.sb(st, f"btb{i}", [128, 16, 64], BF16) for i in range(2)]
            mb = self.sb(st, "mb", [16, 16], F32)
            pts = [self.sb(st, f"pt{i}", [128, 256], BF16) for i in range(6)]
            ptm = [self.sb(st, f"ptm{i}", [16, 64], BF16) for i in range(6)]
            osb32 = self.sb(st, "osb32", [64, TG], F32)
            osbb = [self.sb(st, f"osbb{i}", [64, TG], BF16) for i in range(2)]
            rden = self.sb(st, "rden", [128, TG], F32)
            osb = [osb32] + osbb
            self.dma(mb[:], self.c_mb, writes=["mb"], sem="mb")
            for i in range(2):
                S.op("pool", lambda e, i=i: e.memset(Vxe[i][:, :, 64:128], 1.0), writes=[f"Ve1{i}"])
                S.op("pool", lambda e, i=i: e.memset(Vxo[i][:, :, 64:128], 1.0), writes=[f"Vo1{i}"])
                S.op("pool", lambda e, i=i: e.memset(Vm[i][:, 64:128], 1.0), writes=[f"Vm1{i}"])
            og = 0
            sn = 0
            def loads(h):
                s = h % 2
                self.dma(Kb[s][:, :], sc["K1"][h * 64:(h + 1) * 64, :], writes=[f"Kb{s}"], sem=f"nk{s}")
                self.dma(Qb[s][:, :], sc["Q1"][h * 64:(h + 1) * 64, :], writes=[f"Qb{s}"], sem=f"nq{s}")
                self.dma(Vxe[s][:, :, 0:64], sc["V1"][NMETA:NMETA + 128 * nche, h * 64:(h + 1) * 64].rearrange("(c p) f -> p c f", p=128), writes=[f"Ve{s}"], sem=f"nve{s}")
                self.dma(Vxo[s][:, :, 0:64], sc["V1"][NMETA + 64:NMETA + 64 + 128 * ncho, h * 64:(h + 1) * 64].rearrange("(c p) f -> p c f", p=128), writes=[f"Vo{s}"], sem=f"nvo{s}")
                self.dma(Vm[s][:, 0:64], sc["V1"][0:NMETA, h * 64:(h + 1) * 64], writes=[f"Vm{s}"], sem=f"nvm{s}")
                self.dma(btf[:], self.c_bt[h], writes=["btf"], sem="btf")
                S.op("act", lambda e, s=s: e.activation(out=btb[s][:].rearrange("p a b -> p (a b)"), in_=btf[:], func=AF.Exp), reads=["btf"], writes=[f"btb{s}"])

            loads(0)
            SB = [0, 1, 2, 3, 6, 7]
            for h in range(16):
                s = h % 2
                if h + 1 < 16:
                    loads(h + 1)
                kkeys = [f"Kb{s}", f"Qb{s}"]
                vkeys = [f"Ve{s}", f"Vo{s}", f"Vm{s}", f"Ve1{s}", f"Vo1{s}", f"Vm1{s}"]

                def s_row(r, b):
                    rs = min(max(r - 4, 0), rows - 8)
                    qt = NMETA + r * GW
                    for c in range(4):
                        w = rs + 2 * c
                        kt = NMETA + w * GW
                        S.op("pe", lambda e, c=c, kt=kt: e.matmul(psb[SB[b]][:, 64 * c:64 * c + 64], Kb[s][:, kt:kt + 128], Qb[s][:, qt:qt + 64], start=True, stop=True),
                             reads=kkeys, writes=[f"ps{SB[b]}"])
                    S.op("pe", lambda e: e.matmul(psb[SB[b]][0:16, 256:320], Kb[s][:, 0:16], Qb[s][:, qt:qt + 64], start=True, stop=True), reads=kkeys, writes=[f"ps{SB[b]}"])
                    S.op("act", lambda e: e.activation(out=pts[b][:, 0:256], in_=psb[SB[b]][:, 0:256], func=AF.Exp), reads=[f"ps{SB[b]}"], writes=[f"pt{b}"])
                    S.op("act", lambda e: e.activation(out=ptm[b][0:16, 0:64], in_=psb[SB[b]][0:16, 256:320], func=AF.Exp, bias=mb[0:16, h:h + 1]), reads=[f"ps{SB[b]}", "mb"], writes=[f"ptm{b}"])
                    sl = rs - r + 7
                    S.op("dve", lambda e: e.tensor_tensor(out=pts[b][:, 0:256].rearrange("p (c q) -> p c q", q=64), in0=pts[b][:, 0:256].rearrange("p (c q) -> p c q", q=64),
                                                          in1=btb[s][:, sl:sl + 7:2, :], op=ALU.mult),
                         reads=[f"pt{b}", f"btb{s}"], writes=[f"pt{b}"])

                def pv_row(r, b, po, col):
                    rs = min(max(r - 4, 0), rows - 8)
                    for c in range(4):
                        w = rs + 2 * c
                        vch = Vxe[s][:, w // 2, :] if w % 2 == 0 else Vxo[s][:, (w - 1) // 2, :]
                        S.op("pe", lambda e, c=c, vch=vch: e.matmul(psb[po][:, col:col + 64], vch, pts[b][:, 64 * c:64 * c + 64], start=(c == 0), stop=False),
                             reads=vkeys + [f"pt{b}"], writes=[f"ps{po}"])
                    S.op("pe", lambda e: e.matmul(psb[po][:, col:col + 64], Vm[s][0:16, :], ptm[b][0:16, 0:64], start=False, stop=True),
                         reads=vkeys + [f"ptm{b}"], writes=[f"ps{po}"])

                depth = 4
                bank_of = {}
                for r in range(min(depth, rows)):
                    bank_of[r] = sn % 6
                    sn += 1
                    s_row(r, bank_of[r])
                for r in range(rows):
                    if r + depth < rows:
                        bank_of[r + depth] = sn % 6
                        sn += 1
                        s_row(r + depth, bank_of[r + depth])
                    po = 4 + og % 2
                    pv_row(r, bank_of[r], po, (r % 8) * 64)
                    if r % 8 == 7 or r == rows - 1:
                        r0 = (r // 8) * 8
                        ncol = (r - r0 + 1) * 64
                        self.normalize(ncol, po, og % 2, osb, rden, sc["O"][h * 64:(h + 1) * 64, NMETA + r0 * GW:NMETA + r0 * GW + ncol])
                        og += 1
                b = sn % 6
                sn += 1
                S.op("pe", lambda e: e.matmul(psb[SB[b]][0:16, 0:16], Kb[s][:, 0:16], Qb[s][:, 0:16], start=True, stop=True), reads=kkeys, writes=[f"ps{SB[b]}"])
                S.op("act", lambda e: e.activation(out=ptm[b][0:16, 0:16], in_=psb[SB[b]][0:16, 0:16], func=AF.Exp, bias=mb[0:16, h:h + 1]), reads=[f"ps{SB[b]}", "mb"], writes=[f"ptm{b}"])
                po = 4 + og % 2
                S.op("pe", lambda e: e.matmul(psb[po][:, 0:16], Vm[s][0:16, :], ptm[b][0:16, 0:16], start=True, stop=True), reads=vkeys + [f"ptm{b}"], writes=[f"ps{po}"])
                self.normalize(16, po, og % 2, osb, rden, sc["O"][h * 64:(h + 1) * 64, 0:NMETA])
                og += 1


def _swap16():
    r = np.zeros((128, 128), np.float32)
    for m in range(128):
        k = m + 16 if (m % 32) < 16 else m - 16
        r[k, m] = 1.0
    return r


def _tables(pos, row, col):
    half = 16
    inv = (1.0 / (np.float32(10000.0) ** (np.arange(half, dtype=np.float32) / np.float32(half)))).astype(np.float32)
    p = np.arange(128)
    sgn = np.where((p % 32) < 16, -1.0, 1.0).astype(np.float32)[:, None]
    angM = pos.astype(np.float32)[None, :] * inv[(p % 32) % 16][:, None]
    p64 = p % 64
    use_row = (p64 < 32)[:, None]
    angA = np.where(use_row, row.astype(np.float32)[None, :], col.astype(np.float32)[None, :]) * inv[p64 % 16][:, None]
    angM = angM.astype(np.float32)
    angA = angA.astype(np.float32)
    return (np.cos(angM).astype(np.float32), (sgn * np.sin(angM)).astype(np.float32),
            np.cos(angA).astype(np.float32), (sgn * np.sin(angA)).astype(np.float32))


_CACHE = {}


def kernel(x_prompt, x_sample, meta, norm_gains, ffn1_w_gate, ffn1_w_up, ffn1_w_down, ffn2_w_gate, ffn2_w_up,
           ffn2_w_down, attn_w_in, mla_q_norm, mla_w_uq, mla_kv_norm, mla_w_ukv, gqa_q_norm, gqa_k_norm, attn_w_out,
           na_w_qkv, na_rpb, na_meta_bias, na_w_out, _stages=None, _debug=False):
    f32 = lambda a: np.ascontiguousarray(np.asarray(a, dtype=np.float32))
    x_prompt, x_sample, meta = f32(x_prompt), f32(x_sample), f32(meta)
    nb, seq, _ = x_prompt.shape
    nsb, dseq, _ = x_sample.shape
    assert nb == NCORES and nsb * 4 == NCORES
    rows_p, rows_s = seq // GW, dseq // GW
    key = (rows_p, rows_s, tuple(_stages) if _stages else None)
    if key not in _CACHE:
        b = Builder(rows_p, rows_s)
        b.stages = _stages
        b.build()
        _CACHE[key] = b
    b = _CACHE[key]
    rq, wr = b.rq, b.wr
    shared = {}
    ffw = {"g01": ffn1_w_gate, "u01": ffn1_w_up, "d01": ffn1_w_down, "g02": ffn2_w_gate, "u02": ffn2_w_up, "d02": ffn2_w_down}
    for k, arr in ffw.items():
        arr = f32(arr)
        shared["w32_" + k] = arr[0]
        shared["w32_" + k[0] + "1" + k[2]] = arr[1]
    shared["w32_win"] = f32(attn_w_in)[0]
    perm = [h * 96 + jj for h in range(8) for jj in range(64)] + [h * 96 + 64 + jj for h in range(8) for jj in range(32)]
    shared["w32_wuq"] = np.ascontiguousarray(f32(mla_w_uq)[0][:, perm])
    shared["w32_wukv"] = f32(mla_w_ukv)[0]
    shared["w32_wout"] = f32(attn_w_out)[0]
    shared["w32_wqkv"] = f32(na_w_qkv)[0]
    shared["w32_wnout"] = f32(na_w_out)[0]
    shared["c_ident"] = np.eye(128, dtype=np.float32)
    shared["c_rm"] = _swap16()
    shared["c_ra"] = _swap16()
    ng = f32(norm_gains)
    shared["c_gains"] = np.ascontiguousarray(ng.reshape(2, 6, 8, 128).transpose(3, 0, 1, 2).reshape(128, 96))
    sm = np.zeros((128, 8), np.float32)
    sm[:, 0:2] = f32(mla_q_norm)[0].reshape(2, 128).T
    sm[:, 2] = f32(mla_kv_norm)[0]
    sm[:, 3] = np.tile(f32(gqa_q_norm)[0], 2)
    sm[:, 4] = np.tile(f32(gqa_k_norm)[0], 2)
    shared["c_small"] = sm
    rpb = f32(na_rpb)[0]
    ci = np.arange(GW)
    col_off = np.clip(ci[None, :] - ci[:, None] + 15, 0, 30)
    c_start = np.clip(ci - 8, 0, GW - 16)
    cmask = (ci[None, :] >= c_start[:, None]) & (ci[None, :] < c_start[:, None] + 16)
    rpbc = rpb[:, :, col_off]
    btq = np.where(cmask[None, None], rpbc, np.float32(-1e30)).astype(np.float32)
    bt = btq.transpose(0, 1, 3, 2)
    bt2 = np.zeros((16, 128, 16, 64), np.float32)
    bt2[:, 0:64, 0:15, :] = bt.transpose(0, 2, 1, 3)
    bt2[:, 64:128, 0:14, :] = bt.transpose(0, 2, 1, 3)[:, :, 1:15, :]
    shared["c_bt"] = np.ascontiguousarray(bt2.reshape(16, 128, 16 * 64))
    shared["c_mb"] = np.ascontiguousarray(f32(na_meta_bias)[0].T)
    tokP = np.arange(b.LP)
    gridP = np.maximum(tokP - NMETA, 0)
    rowP = np.where(tokP < NMETA, -1, gridP // GW)
    colP = np.where(tokP < NMETA, tokP, gridP % GW)
    tabP = _tables(tokP, rowP, colP)
    in_maps = []
    w0s = []
    for c in range(NCORES):
        m = dict(shared)
        m["xP"] = np.concatenate([meta, x_prompt[c]], axis=0)
        s, qi = c // 4, c % 4
        w0 = int(np.clip(rq * qi - 4, 0, rows_s - wr))
        w0s.append(w0)
        win = np.arange(w0 * GW, (w0 + wr) * GW)
        rest = np.concatenate([np.arange(0, w0 * GW), np.arange((w0 + wr) * GW, rows_s * GW)])
        order = np.concatenate([win, rest])
        m["xS"] = np.concatenate([meta, x_sample[s][order]], axis=0)
        pos = np.concatenate([np.arange(NMETA), NMETA + order])
        row = np.concatenate([np.full(NMETA, -1), order // GW])
        col = np.concatenate([np.arange(NMETA), order % GW])
        tabS = _tables(pos, row, col)
        for nm, tp, ts in zip(("cosM", "sinM", "cosA", "sinA"), tabP, tabS):
            m[nm + "P"] = tp
            m[nm + "S"] = ts
        in_maps.append(m)
    res = run_bass_kernel_spmd(b.nc, in_maps, core_ids=list(range(NCORES)))
    y_prompt = np.stack([np.asarray(res.results[c]["yP"], dtype=np.float32) for c in range(NCORES)], axis=0)
    y_sample = np.zeros((nsb, dseq, D), np.float32)
    for c in range(NCORES):
        s, qi = c // 4, c % 4
        ys = np.asarray(res.results[c]["yS"], dtype=np.float32)
        lo = (rq * qi - w0s[c]) * GW
        y_sample[s, rq * qi * GW:rq * (qi + 1) * GW] = ys[lo:lo + rq * GW]
    if _debug:
        return (y_prompt, y_sample), res
    return (y_prompt, y_sample)
```
